# Optimizing a Trainium2 kernel written in Bass

```python
import jax, jax.numpy as jnp
from jax import lax
import numpy as np

D_MODEL = 2048
BATCH = 4
SEQ = 4096
DEPTH = 2

EPS = 1e-6
MLA_HEADS = 8
Q_LORA_RANK = 512
KV_LORA_RANK = 512
QK_NOPE_DIM = 128
QK_ROPE_DIM = 64
QK_HEAD_DIM = QK_NOPE_DIM + QK_ROPE_DIM
V_HEAD_DIM = 128
ROPE_THETA = 10000.0
Q_BLOCK = 128
MLA_KV_IN = KV_LORA_RANK + QK_ROPE_DIM
CONV_GROUPS = 8
CONV_GROUP_WIDTH = 64
CONV_CHANNELS = CONV_GROUPS * CONV_GROUP_WIDTH
CONV_WIDTH = 31
GDN_HEADS = 4
GDN_K_DIM = 128
GDN_V_DIM = 128
GDN_SHORT_CONV = 5
GDN_CHUNK = 64
GDN_QK_W = GDN_HEADS * GDN_K_DIM
GDN_V_W = GDN_HEADS * GDN_V_DIM
D_IN = Q_LORA_RANK + MLA_KV_IN + 2 * CONV_CHANNELS + 2 * GDN_QK_W + 2 * GDN_V_W + 4 * GDN_HEADS
D_MIX = MLA_HEADS * V_HEAD_DIM + CONV_CHANNELS + GDN_V_W
D_FF = -(-8 * D_MODEL // (3 * 256)) * 256

kernel_name = "hybrid_mla_conformer_gdn_encoder"


def rmsnorm(x, w):
    xf = x.astype(jnp.float32)
    y = xf * lax.rsqrt(jnp.mean(xf * xf, axis=-1, keepdims=True) + EPS)
    return (y * w.astype(jnp.float32)).astype(x.dtype)


def layernorm(x, g, b):
    xf = x.astype(jnp.float32)
    mu = jnp.mean(xf, axis=-1, keepdims=True)
    var = jnp.mean(jnp.square(xf - mu), axis=-1, keepdims=True)
    y = (xf - mu) * lax.rsqrt(var + EPS)
    return (y * g.astype(jnp.float32) + b.astype(jnp.float32)).astype(x.dtype)


def l2norm(x):
    return x * lax.rsqrt(jnp.sum(x * x, axis=-1, keepdims=True) + EPS)


def rope_tables(positions):
    inv = 1.0 / (ROPE_THETA ** (jnp.arange(0, QK_ROPE_DIM, 2, dtype=jnp.float32) / QK_ROPE_DIM))
    ang = positions.astype(jnp.float32)[..., None] * inv
    return jnp.cos(ang), jnp.sin(ang)


def apply_rope(t, cos, sin):
    half = t.shape[-1] // 2
    tf = t.astype(jnp.float32)
    t1, t2 = tf[..., :half], tf[..., half:]
    return jnp.concatenate([t1 * cos - t2 * sin, t2 * cos + t1 * sin], axis=-1).astype(t.dtype)


def depthwise_conv(h, w):
    pad = w.shape[0] // 2
    return lax.conv_general_dilated(h, w.astype(h.dtype), window_strides=(1,), padding=[(pad, pad)],
                                    dimension_numbers=('NWC', 'WIO', 'NWC'),
                                    feature_group_count=h.shape[-1])


def mla_mixer(c_q, c_kv_rope, q_a_norm, w_uq, kv_a_norm, w_ukv, cos, sin):
    B, S, _ = c_q.shape
    q = (rmsnorm(c_q, q_a_norm) @ w_uq).reshape(B, S, MLA_HEADS, QK_HEAD_DIM)
    q_nope = q[..., :QK_NOPE_DIM]
    q_rope = apply_rope(q[..., QK_NOPE_DIM:], cos[:, :, None, :], sin[:, :, None, :])
    c_kv = c_kv_rope[..., :KV_LORA_RANK]
    k_rope = apply_rope(c_kv_rope[..., KV_LORA_RANK:], cos, sin)
    kv = (rmsnorm(c_kv, kv_a_norm) @ w_ukv).reshape(B, S, MLA_HEADS, QK_NOPE_DIM + V_HEAD_DIM)
    k_nope, v = kv[..., :QK_NOPE_DIM], kv[..., QK_NOPE_DIM:]
    scale = QK_HEAD_DIM ** -0.5
    nb = S // Q_BLOCK

    def blocks(t):
        return jnp.moveaxis(t.reshape(B, nb, Q_BLOCK, *t.shape[2:]), 1, 0)

    def attend(qs):
        qn, qr = qs
        s = (jnp.einsum('bqhd,bkhd->bhqk', qn, k_nope, preferred_element_type=jnp.float32)
             + jnp.einsum('bqhr,bkr->bhqk', qr, k_rope, preferred_element_type=jnp.float32)) * scale
        p = jax.nn.softmax(s, axis=-1).astype(v.dtype)
        return jnp.einsum('bhqk,bkhd->bqhd', p, v)

    o = lax.map(attend, (blocks(q_nope), blocks(q_rope)))
    return jnp.moveaxis(o, 0, 1).reshape(B, S, MLA_HEADS * V_HEAD_DIM)


def conformer_conv(u, dw_w, dw_b, ln_g, ln_b):
    a, gate = jnp.split(u, 2, axis=-1)
    h = a * jax.nn.sigmoid(gate)
    h = depthwise_conv(h, dw_w) + dw_b.astype(h.dtype)
    h = layernorm(h, ln_g, ln_b)
    return jax.nn.silu(h)


def chunk_gated_delta(q, k, v, g, beta):
    B, S, H, K = q.shape
    V = v.shape[-1]
    C = GDN_CHUNK
    N = S // C

    def chunks(t):
        t = t.reshape(B, N, C, H, *t.shape[3:])
        return jnp.moveaxis(jnp.moveaxis(t, 1, 0), 2, 3)

    q, k, v, g, beta = chunks(q), chunks(k), chunks(v), chunks(g), chunks(beta)
    gc = jnp.cumsum(g, axis=-1)
    kb = k * beta[..., None]
    vb = v * beta[..., None]
    idx = jnp.arange(C)
    incl = idx[:, None] >= idx[None, :]
    strict = idx[:, None] > idx[None, :]
    diff = gc[..., :, None] - gc[..., None, :]
    decay = jnp.where(incl, jnp.exp(jnp.where(incl, diff, 0.0)), 0.0)
    a_mat = jnp.where(strict, jnp.einsum('nbhik,nbhjk->nbhij', kb, k) * decay, 0.0)
    eye = jnp.eye(C, dtype=jnp.float32)
    t_inv = lax.linalg.triangular_solve(eye + a_mat, jnp.broadcast_to(eye, a_mat.shape),
                                        left_side=True, lower=True, unit_diagonal=True)
    u = jnp.einsum('nbhij,nbhjv->nbhiv', t_inv, vb)
    w = jnp.einsum('nbhij,nbhjk->nbhik', t_inv, kb * jnp.exp(gc)[..., None])
    qk = jnp.einsum('nbhik,nbhjk->nbhij', q, k) * decay

    def step(state, xs):
        q_c, k_c, u_c, w_c, qk_c, g_c = xs
        v_new = u_c - jnp.einsum('bhck,bhkv->bhcv', w_c, state)
        o_c = (jnp.einsum('bhck,bhkv->bhcv', q_c * jnp.exp(g_c)[..., None], state)
               + jnp.einsum('bhij,bhjv->bhiv', qk_c, v_new))
        g_last = g_c[..., -1:]
        state = (state * jnp.exp(g_last)[..., None]
                 + jnp.einsum('bhck,bhcv->bhkv', k_c * jnp.exp(g_last - g_c)[..., None], v_new))
        return state, o_c

    state0 = jnp.zeros((B, H, K, V), jnp.float32)
    _, o = lax.scan(step, state0, (q, k, u, w, qk, gc))
    return jnp.moveaxis(jnp.moveaxis(o, 3, 2), 0, 1).reshape(B, S, H, V)


def gdn_mixer(q, k, v, z, a_f, b_f, a_b, b_b, conv_w, a_log, dt_bias, out_norm):
    B, S, _ = q.shape
    f32 = jnp.float32
    qkv = jax.nn.silu(depthwise_conv(jnp.concatenate([q, k, v], axis=-1), conv_w))
    q, k, v = jnp.split(qkv, [GDN_QK_W, 2 * GDN_QK_W], axis=-1)
    q = l2norm(q.reshape(B, S, GDN_HEADS, GDN_K_DIM).astype(f32)) * (GDN_K_DIM ** -0.5)
    k = l2norm(k.reshape(B, S, GDN_HEADS, GDN_K_DIM).astype(f32))
    v = v.reshape(B, S, GDN_HEADS, GDN_V_DIM).astype(f32)

    def gates(a, b, a_log_d, dt_bias_d):
        g = -jnp.exp(a_log_d.astype(f32)) * jax.nn.softplus(a.astype(f32) + dt_bias_d.astype(f32))
        return g, jax.nn.sigmoid(b.astype(f32))

    g_f, beta_f = gates(a_f, b_f, a_log[0], dt_bias[0])
    g_b, beta_b = gates(a_b, b_b, a_log[1], dt_bias[1])
    o_f = chunk_gated_delta(q, k, v, g_f, beta_f)
    o_b = jnp.flip(chunk_gated_delta(jnp.flip(q, 1), jnp.flip(k, 1), jnp.flip(v, 1),
                                     jnp.flip(g_b, 1), jnp.flip(beta_b, 1)), 1)
    o = rmsnorm(o_f + o_b, out_norm) * jax.nn.silu(z.reshape(B, S, GDN_HEADS, GDN_V_DIM).astype(f32))
    return o.reshape(B, S, GDN_V_W).astype(z.dtype)


def hybrid_layer(x, cos, sin, pre_mix_norm, w_in, q_a_norm, w_uq, kv_a_norm, w_ukv,
                 conv_dw_w, conv_dw_b, conv_ln_g, conv_ln_b, gdn_conv_w, gdn_a_log, gdn_dt_bias,
                 gdn_out_norm, w_out, post_mix_norm, pre_ffn_norm, w_gate, w_up, w_down, post_ffn_norm):
    h = rmsnorm(x, pre_mix_norm)
    proj = h @ w_in
    sizes = (Q_LORA_RANK, MLA_KV_IN, 2 * CONV_CHANNELS, GDN_QK_W, GDN_QK_W, GDN_V_W, GDN_V_W,
             GDN_HEADS, GDN_HEADS, GDN_HEADS, GDN_HEADS)
    c_q, c_kv, conv_in, g_q, g_k, g_v, g_z, a_f, b_f, a_b, b_b = jnp.split(
        proj, np.cumsum(sizes)[:-1].tolist(), axis=-1)
    o_a = mla_mixer(c_q, c_kv, q_a_norm, w_uq, kv_a_norm, w_ukv, cos, sin)
    o_b = conformer_conv(conv_in, conv_dw_w, conv_dw_b, conv_ln_g, conv_ln_b)
    o_c = gdn_mixer(g_q, g_k, g_v, g_z, a_f, b_f, a_b, b_b, gdn_conv_w, gdn_a_log, gdn_dt_bias, gdn_out_norm)
    mix = jnp.concatenate([o_a, o_b.astype(o_a.dtype), o_c.astype(o_a.dtype)], axis=-1) @ w_out
    x = x + rmsnorm(mix, post_mix_norm)
    h = rmsnorm(x, pre_ffn_norm)
    f = (jax.nn.silu(h @ w_gate) * (h @ w_up)) @ w_down
    return x + rmsnorm(f, post_ffn_norm)


def setup_inputs(seed: int = 0) -> dict:
    key = jax.random.key(seed)
    ks = iter(jax.random.split(key, 32))
    L = DEPTH
    f32 = jnp.float32

    def nrm(shape, scale):
        return jax.random.normal(next(ks), shape, f32) * scale

    def gain(n):
        return 1.0 + 0.01 * jax.random.normal(next(ks), (L, n), f32)

    x = jax.random.normal(next(ks), (BATCH, SEQ, D_MODEL), f32)
    positions = (jnp.arange(SEQ, dtype=jnp.int32)[None, :]
                 + jax.random.randint(next(ks), (BATCH, 1), 0, 1024, dtype=jnp.int32))
    pre_mix_norm = gain(D_MODEL)
    w_in = nrm((L, D_MODEL, D_IN), D_MODEL ** -0.5)
    q_a_norm = gain(Q_LORA_RANK)
    w_uq = nrm((L, Q_LORA_RANK, MLA_HEADS * QK_HEAD_DIM), Q_LORA_RANK ** -0.5)
    kv_a_norm = gain(KV_LORA_RANK)
    w_ukv = nrm((L, KV_LORA_RANK, MLA_HEADS * (QK_NOPE_DIM + V_HEAD_DIM)), KV_LORA_RANK ** -0.5)
    conv_dw_w = nrm((L, CONV_WIDTH, 1, CONV_CHANNELS), CONV_WIDTH ** -0.5)
    conv_dw_b = nrm((L, CONV_CHANNELS), 0.01)
    conv_ln_g = gain(CONV_CHANNELS)
    conv_ln_b = nrm((L, CONV_CHANNELS), 0.01)
    gdn_conv_w = nrm((L, GDN_SHORT_CONV, 1, 2 * GDN_QK_W + GDN_V_W), GDN_SHORT_CONV ** -0.5)
    gdn_a_log = jnp.log(jax.random.uniform(next(ks), (L, 2, GDN_HEADS), f32, minval=1.0, maxval=16.0))
    dt = jnp.exp(jax.random.uniform(next(ks), (L, 2, GDN_HEADS), f32,
                                    minval=float(np.log(1e-3)), maxval=float(np.log(1e-1))))
    gdn_dt_bias = dt + jnp.log(-jnp.expm1(-dt))
    gdn_out_norm = gain(GDN_V_DIM)
    w_out = nrm((L, D_MIX, D_MODEL), D_MIX ** -0.5)
    post_mix_norm = gain(D_MODEL)
    pre_ffn_norm = gain(D_MODEL)
    w_gate = nrm((L, D_MODEL, D_FF), D_MODEL ** -0.5)
    w_up = nrm((L, D_MODEL, D_FF), D_MODEL ** -0.5)
    w_down = nrm((L, D_FF, D_MODEL), D_FF ** -0.5)
    post_ffn_norm = gain(D_MODEL)
    return {"x": x, "positions": positions, "pre_mix_norm": pre_mix_norm, "w_in": w_in,
            "q_a_norm": q_a_norm, "w_uq": w_uq, "kv_a_norm": kv_a_norm, "w_ukv": w_ukv,
            "conv_dw_w": conv_dw_w, "conv_dw_b": conv_dw_b, "conv_ln_g": conv_ln_g,
            "conv_ln_b": conv_ln_b, "gdn_conv_w": gdn_conv_w, "gdn_a_log": gdn_a_log,
            "gdn_dt_bias": gdn_dt_bias, "gdn_out_norm": gdn_out_norm, "w_out": w_out,
            "post_mix_norm": post_mix_norm, "pre_ffn_norm": pre_ffn_norm, "w_gate": w_gate,
            "w_up": w_up, "w_down": w_down, "post_ffn_norm": post_ffn_norm}


def reference(x, positions, pre_mix_norm, w_in, q_a_norm, w_uq, kv_a_norm, w_ukv, conv_dw_w,
              conv_dw_b, conv_ln_g, conv_ln_b, gdn_conv_w, gdn_a_log, gdn_dt_bias, gdn_out_norm,
              w_out, post_mix_norm, pre_ffn_norm, w_gate, w_up, w_down, post_ffn_norm):
    cos, sin = rope_tables(positions)
    for l in range(DEPTH):
        x = hybrid_layer(x, cos, sin, pre_mix_norm[l], w_in[l], q_a_norm[l], w_uq[l], kv_a_norm[l],
                         w_ukv[l], conv_dw_w[l], conv_dw_b[l], conv_ln_g[l], conv_ln_b[l],
                         gdn_conv_w[l], gdn_a_log[l], gdn_dt_bias[l], gdn_out_norm[l], w_out[l],
                         post_mix_norm[l], pre_ffn_norm[l], w_gate[l], w_up[l], w_down[l],
                         post_ffn_norm[l])
    return x
```

```python
import contextlib
import math
import numpy as np
from concourse.bass_utils import run_bass_kernel_spmd
import concourse.bass as bass
import concourse.mybir as mybir

F32 = mybir.dt.float32
BF16 = mybir.dt.bfloat16
I32 = mybir.dt.int32
AF = mybir.ActivationFunctionType
ALU = mybir.AluOpType
AX = mybir.AxisListType

ENGS = ("pe", "act", "dve", "pool", "sp")


def _prod(xs):
    r = 1
    for v in xs:
        r *= int(v)
    return r


def bbox(ap):
    t = ap.tensor
    pairs = ap.ap
    off = int(ap.offset)
    if str(ap.space) == "DRAM":
        ext = 0
        for s, c in pairs:
            ext += abs(int(s)) * (int(c) - 1)
        return (t.name, 0, 1, off, off + ext + 1)
    pstride = _prod(t.shape[1:])
    p0 = off // pstride
    f0 = off % pstride
    np_ = int(pairs[0][1]) if int(pairs[0][0]) != 0 else 1
    ext = 0
    for s, c in pairs[1:]:
        ext += abs(int(s)) * (int(c) - 1)
    return (t.name, p0, p0 + np_, f0, f0 + ext + 1)


def _ovl(a, b):
    return a[1] < b[2] and b[1] < a[2] and a[3] < b[4] and b[3] < a[4]


class _Op:
    __slots__ = ("eng", "pos", "fn", "waits", "clock", "signal", "sigidx", "dma", "cover")


class Sched:
    def __init__(self, nc, n_dma_sems=40, same_engine_sync=True):
        self.nc = nc
        self.ops = {e: [] for e in ENGS}
        self.known = {e: {} for e in ENGS}
        self.acc = {}
        self.n_dma = n_dma_sems
        self.dma_last = [None] * n_dma_sems
        self.dma_cnt = [0] * n_dma_sems
        self.dma_rr = 0
        self.dma_rr_sw = 0
        self.n_hw = (n_dma_sems * 5) // 8
        self.same_engine_sync = same_engine_sync
        self.all_dma_ops = []
        self.last_compute = {}

    def _key(self, op):
        if op.dma is not None:
            return ("d", op.dma[0])
        return op.eng

    def _val(self, op):
        if op.dma is not None:
            return op.dma[1]
        return op.pos

    def add(self, eng, fn, reads=(), writes=(), dma=False, cc=False, deps_extra=()):
        op = _Op()
        op.eng = eng
        op.fn = fn
        op.signal = False
        op.sigidx = None
        op.dma = None
        lst = self.ops[eng]
        op.pos = len(lst) + 1
        deps = list(deps_extra)
        rb = [bbox(a) for a in reads]
        wb = [bbox(a) for a in writes]
        for b in rb:
            rec = self.acc.get(b[0])
            if rec is None:
                continue
            for (wbx, wop) in rec["w"]:
                if _ovl(b, wbx):
                    deps.append(wop)
        for b in wb:
            rec = self.acc.get(b[0])
            if rec is None:
                continue
            for (wbx, wop) in rec["w"]:
                if _ovl(b, wbx):
                    deps.append(wop)
            for (k, rop) in rec["r"].items():
                if _ovl(b, k[1]):
                    deps.append(rop)
        if cc:
            k = len(self.dma_last)
            self.dma_last.append(op)
            self.dma_cnt.append(1)
            op.dma = (k, 1)
        elif dma:
            if eng == "pool":
                k = self.n_hw + self.dma_rr_sw
                self.dma_rr_sw = (self.dma_rr_sw + 1) % (self.n_dma - self.n_hw)
            else:
                k = self.dma_rr
                self.dma_rr = (self.dma_rr + 1) % self.n_hw
            if self.dma_last[k] is not None:
                deps.append(self.dma_last[k])
            self.dma_cnt[k] += 1
            op.dma = (k, 16 * self.dma_cnt[k])
            self.dma_last[k] = op
            self.all_dma_ops.append(op)
        known = self.known[eng]
        waits = {}
        for d in deps:
            if d is op:
                continue
            key = self._key(d)
            val = self._val(d)
            if d.dma is None and d.eng == eng:
                if eng == "pe" or not self.same_engine_sync:
                    continue
            if known.get(key, 0) >= val:
                continue
            prev = waits.get(key)
            if prev is None or self._val(prev) < val:
                waits[key] = d
        final = []
        for key, d in waits.items():
            val = self._val(d)
            implied = False
            for key2, d2 in waits.items():
                if d2 is d:
                    continue
                if d2.clock.get(key, 0) >= val:
                    implied = True
                    break
            if not implied:
                final.append(d)
        for d in final:
            d.signal = True
        for d in waits.values():
            for k2, v2 in d.clock.items():
                if known.get(k2, 0) < v2:
                    known[k2] = v2
            key = self._key(d)
            val = self._val(d)
            if known.get(key, 0) < val:
                known[key] = val
        op.waits = final
        clock = dict(known)
        if op.dma is None:
            clock[eng] = op.pos
            known[eng] = op.pos if eng == "pe" else known.get(eng, 0)
        op.clock = clock
        lst.append(op)
        if op.dma is None:
            self.last_compute[eng] = op
        for b in rb:
            rec = self.acc.setdefault(b[0], {"w": [], "r": {}})
            rec["r"][(eng if op.dma is None else ("d", op.dma[0], op.dma[1]), b)] = op
        for b in wb:
            rec = self.acc.setdefault(b[0], {"w": [], "r": {}})
            rec["w"] = [(x, o) for (x, o) in rec["w"]
                        if not (b[1] <= x[1] and x[2] <= b[2] and b[3] <= x[3] and x[4] <= b[4])]
            rec["r"] = {k: o for (k, o) in rec["r"].items()
                        if not (b[1] <= k[1][1] and k[1][2] <= b[2] and b[3] <= k[1][3] and k[1][4] <= b[4])}
            rec["w"].append((b, op))
        return op

    def barrier(self):
        lasts = [o for o in self.last_compute.values()]
        lasts += [d for d in self.dma_last if d is not None]
        for e in ENGS:
            self.add(e, lambda eng: eng.nop(), deps_extra=lasts)
        self.acc = {}

    def emit(self):
        nc = self.nc
        for k in range(len(self.dma_last)):
            if self.dma_last[k] is not None:
                self.dma_last[k].signal = True
        with contextlib.ExitStack() as es:
            sems = {e: es.enter_context(nc.semaphore("sem_" + e)) for e in ENGS}
            dsems = [es.enter_context(nc.semaphore("dsem%d" % k)) for k in range(len(self.dma_last))]
            for e in ENGS:
                n = 0
                for op in self.ops[e]:
                    if op.dma is None and op.signal:
                        n += 1
                        op.sigidx = n
            block = es.enter_context(nc.Block())

            def replay(eobj, ename, final=False):
                for op in self.ops[ename]:
                    for d in op.waits:
                        if d.dma is not None:
                            eobj.wait_ge(dsems[d.dma[0]], d.dma[1])
                        else:
                            eobj.wait_ge(sems[d.eng], d.sigidx)
                    ins = op.fn(eobj)
                    if op.dma is not None:
                        if op.dma[0] >= self.n_dma:
                            ins.then_inc(dsems[op.dma[0]])
                        else:
                            ins.then_inc(dsems[op.dma[0]], 16)
                    elif op.signal:
                        ins.then_inc(sems[ename], 1)
                if final:
                    for k in range(len(self.dma_last)):
                        if self.dma_last[k] is not None:
                            eobj.wait_ge(dsems[k], self.dma_last[k].dma[1])

            @block.tensor
            def _(e):
                replay(e, "pe")

            @block.scalar
            def _(e):
                replay(e, "act")

            @block.vector
            def _(e):
                replay(e, "dve")

            @block.gpsimd
            def _(e):
                replay(e, "pool")

            @block.sync
            def _(e):
                replay(e, "sp", final=True)

    def mm(self, out, lhsT, rhs, start=True, stop=True, **kw):
        return self.add("pe", lambda e: e.matmul(out, lhsT=lhsT, rhs=rhs, start=start, stop=stop, **kw),
                        reads=[lhsT, rhs], writes=[out])

    def transpose(self, out, in_, ident):
        return self.add("pe", lambda e: e.transpose(out, in_, ident), reads=[in_, ident], writes=[out])

    def act(self, out, in_, func, bias=None, scale=None, accum_out=None, eng="act"):
        reads = [in_]
        kw = {}
        if bias is not None:
            kw["bias"] = bias
            if not isinstance(bias, (int, float)):
                reads.append(bias)
        if scale is not None:
            kw["scale"] = scale
            if not isinstance(scale, (int, float)):
                reads.append(scale)
        writes = [out]
        if accum_out is not None:
            kw["accum_out"] = accum_out
            writes.append(accum_out)
        return self.add(eng, lambda e: e.activation(out, in_, func, **kw), reads=reads, writes=writes)

    def copy(self, eng, out, in_):
        if eng == "act":
            return self.add("act", lambda e: e.copy(out, in_), reads=[in_], writes=[out])
        return self.add(eng, lambda e: e.tensor_copy(out, in_), reads=[in_], writes=[out])

    def tt(self, eng, out, in0, in1, op):
        return self.add(eng, lambda e: e.tensor_tensor(out, in0, in1, op), reads=[in0, in1], writes=[out])

    def ts(self, eng, out, in0, s1, op0, s2=None, op1=None, accum_out=None):
        reads = [in0]
        if not isinstance(s1, (int, float)):
            reads.append(s1)
        if s2 is not None and not isinstance(s2, (int, float)):
            reads.append(s2)
        writes = [out]
        kw = {}
        if op1 is not None:
            kw["op1"] = op1
        if accum_out is not None:
            kw["accum_out"] = accum_out
            writes.append(accum_out)
        return self.add(eng, lambda e: e.tensor_scalar(out, in0, s1, s2, op0, **kw), reads=reads, writes=writes)

    def stt(self, eng, out, in0, scalar, in1, op0, op1):
        reads = [in0, in1]
        if not isinstance(scalar, (int, float)):
            reads.append(scalar)
        return self.add(eng, lambda e: e.scalar_tensor_tensor(out, in0, scalar, in1, op0, op1),
                        reads=reads, writes=[out])

    def memset(self, eng, ap, val):
        return self.add(eng, lambda e: e.memset(ap, val), writes=[ap])

    def recip(self, out, in_):
        return self.add("dve", lambda e: e.reciprocal(out, in_), reads=[in_], writes=[out])

    def reduce(self, eng, out, in_, op, axis=None):
        ax = AX.X if axis is None else axis
        return self.add(eng, lambda e: e.tensor_reduce(out, in_, ax, op), reads=[in_], writes=[out])

    def dma(self, q, out, in_, **kw):
        return self.add(q, lambda e: e.dma_start(out=out, in_=in_, **kw), reads=[in_], writes=[out], dma=True)


D_MODEL = 2048
BATCH = 4
SEQ = 4096
DEPTH = 2
T = 2048
NTB = 4
EPS = 1e-6
SCALE = 192.0 ** -0.5
D_FF = 5632
NFF = 44
C_CQ, C_CKV, C_KR, C_KRR, C_CA, C_CG, C_GQ, C_GK, C_GV, C_GZ, C_GT = (
    0, 512, 1024, 1088, 1152, 1664, 2176, 2688, 3200, 3712, 4224)
W_IN_COLS = 4240
V_PRE_MIX, V_POST_MIX, V_PRE_FFN, V_POST_FFN = 0, 16, 32, 48
V_QA, V_KVA, V_CDB, V_CLG, V_CLB = 64, 68, 72, 76, 80
V_CDW = 84
V_GCW = V_CDW + 124
NV = V_GCW + 60
K_ID = 0
K_TRIA = 128
K_TRIB = 256
K_BLK0 = 384
K_BLK1 = 512
K_MSA = 640
K_MIA = 768
K_MSB = 896
K_MIB = 1024
K_INVF = 1152
K_SGN = 1153
NC = 1154


class Ctx:
    pass


@contextlib.contextmanager
def pool(k):
    es = contextlib.ExitStack()
    cnt = [0]

    def sb(shape, dt, name="t"):
        k.uid += 1
        return es.enter_context(k.nc.sbuf_tensor("%s_%d" % (name, k.uid), list(shape), dt))

    def ps(shape, dt=F32, name="p"):
        k.uid += 1
        return es.enter_context(k.nc.psum_tensor("%s_%d" % (name, k.uid), list(shape), dt))

    try:
        yield sb, ps
        k.S.barrier()
    finally:
        es.close()


def rstd_from_ss(k, out_sb, ss_ps, inv_n):
    S = k.S
    S.act(out_sb, ss_ps, AF.Sqrt, bias=k.eps_col[0:out_sb.partition_size(), 0:1], scale=inv_n)
    S.recip(out_sb, out_sb)


def stage_setup(k):
    S, nc = k.S, k.nc
    sb = k.gsb
    k.consts = sb([128, NC], F32, "consts")
    S.dma("sp", k.consts[:], k.d_consts[:, :])
    k.ident_bf = sb([128, 128], BF16, "identbf")
    S.copy("dve", k.ident_bf[:], k.consts[:, K_ID:K_ID + 128])
    k.ones_bf = sb([128, 128], BF16, "onesbf")
    S.memset("dve", k.ones_bf[:], 1.0)
    k.ones_f = sb([128, 128], F32, "onesf")
    S.memset("dve", k.ones_f[:], 1.0)
    k.eps_col = sb([128, 1], F32, "epscol")
    S.memset("dve", k.eps_col[:], EPS)
    k.flags = sb([128, 2], F32, "flags")
    S.dma("sp", k.flags[:], k.d_flags[:, :])
    k.vecs = []
    for l in range(DEPTH):
        v = sb([128, NV], F32, "vecs%d" % l)
        S.dma("sp", v[:], k.d_vecs[l, :, :])
        k.vecs.append(v)
    k.gpar = []
    k.gon = []
    for l in range(DEPTH):
        g = sb([128, 16], F32, "gpar%d" % l)
        S.dma("sp", g[:], k.d_gpar[l, :, :])
        k.gpar.append(g)
        o = sb([128, 128], F32, "gon%d" % l)
        S.dma("sp", o[:], k.d_gon[l, :, :])
        k.gon.append(o)


def build_rope(k, sb):
    S = k.S
    k.cos2 = sb([64, T], F32, "cos2")
    k.sin2 = sb([64, T], F32, "sin2")
    with pool(k) as (tsb, tps):
        posi = tsb([64, T], I32, "posi")
        S.dma("sp", posi[:], k.d_pos[:, :])
        ang = tsb([64, T], F32, "ang")
        S.copy("dve", ang[:], posi[:])
        S.ts("dve", ang[:], ang[:], k.consts[0:64, K_INVF:K_INVF + 1], ALU.mult)
        tmp = tsb([64, T], F32, "tmp")
        nf = tsb([64, T], F32, "nf")
        ni = tsb([64, T], I32, "ni")
        msk = tsb([64, T], F32, "msk")
        two_pi = 2.0 * math.pi
        C1 = 6.28125
        C2 = two_pi - C1

        def reduced_sin(out, shift):
            S.ts("dve", tmp[:], ang[:], shift, ALU.add, 1.0 / two_pi, ALU.mult)
            S.copy("dve", ni[:], tmp[:])
            S.copy("dve", nf[:], ni[:])
            S.ts("dve", tmp[:], ang[:], shift, ALU.add)
            S.stt("dve", tmp[:], nf[:], -C1, tmp[:], ALU.mult, ALU.add)
            S.stt("dve", tmp[:], nf[:], -C2, tmp[:], ALU.mult, ALU.add)
            S.ts("dve", msk[:], tmp[:], math.pi, ALU.is_gt, -two_pi, ALU.mult)
            S.tt("dve", tmp[:], tmp[:], msk[:], ALU.add)
            S.ts("dve", msk[:], tmp[:], -math.pi, ALU.is_lt, two_pi, ALU.mult)
            S.tt("dve", tmp[:], tmp[:], msk[:], ALU.add)
            S.ts("dve", tmp[:], tmp[:], math.pi, ALU.min, -math.pi, ALU.max)
            S.act(out, tmp[:], AF.Sin)
        reduced_sin(k.sin2[:], 0.0)
        reduced_sin(k.cos2[:], math.pi / 2.0)
        S.ts("dve", k.sin2[:], k.sin2[:], k.consts[0:64, K_SGN:K_SGN + 1], ALU.mult)


def norm_to_hT(k, sb, ps, x_src, gain_ap, hT):
    S = k.S
    xbs = [sb([128, 16, 512], F32, "xb") for _ in range(2)]
    sq = sb([128, 16, 512], BF16, "sq")
    ss = [ps([128, 512], F32, "ss") for _ in range(2)]
    rstd = [sb([128, 512], F32, "rstd") for _ in range(2)]
    for tb in range(NTB):
        xb = xbs[tb % 2]
        tsl = slice(tb * 512, (tb + 1) * 512)
        S.dma("sp", xb[:], x_src[:, :, tsl].rearrange("c p t -> p c t"))
        S.act(sq[:], xb[:], AF.Square)
        for c in range(16):
            S.mm(ss[tb % 2][:], k.ones_bf[:], sq[:, c, :], start=(c == 0), stop=(c == 15))
        rstd_from_ss(k, rstd[tb % 2][:], ss[tb % 2][:], 1.0 / D_MODEL)
        for c in range(16):
            S.stt("dve", hT[:, c, tsl], xb[:, c, :], gain_ap[:, c:c + 1], rstd[tb % 2][:], ALU.mult, ALU.mult)


def stream_fm(k, sb, ps, w_view, col0, ncols_blk, nblk, hT, kchunks, consumer, wname="wb", m=128, nbuf=3):
    S = k.S
    wbs = [sb([128, kchunks, m], BF16, wname) for _ in range(nbuf)]
    pss = [ps([128, 512], F32, "pfm") for _ in range(2)]
    n = 0
    for j in range(nblk):
        wb = wbs[j % nbuf]
        c0 = col0 + j * m
        S.dma("pool", wb[:], w_view[:, c0:c0 + m].rearrange("(kc p) n -> p kc n", p=128))
        for tb in range(NTB):
            pt = pss[n % 2]
            n += 1
            for kc in range(kchunks):
                S.mm(pt[0:m, :], wb[:, kc, :], hT[:, kc, tb * 512:(tb + 1) * 512],
                     start=(kc == 0), stop=(kc == kchunks - 1))
            consumer(j, tb, pt[0:m, :])


def norm512(k, sb, ps, raw, gain_ap, out):
    S = k.S
    sq = sb([128, 4, 512], BF16, "sq5")
    ss = ps([128, 512], F32, "ss5")
    rstd = sb([128, 512], F32, "rstd5")
    for tb in range(NTB):
        tsl = slice(tb * 512, (tb + 1) * 512)
        S.act(sq[:], raw[:, :, tsl], AF.Square)
        for c in range(4):
            S.mm(ss[:], k.ones_bf[:], sq[:, c, :], start=(c == 0), stop=(c == 3))
        rstd_from_ss(k, rstd[:], ss[:], 1.0 / 512.0)
        for c in range(4):
            S.stt("dve", out[:, c, tsl], raw[:, c, tsl], gain_ap[:, c:c + 1], rstd[:], ALU.mult, ALU.mult)


PROJ_PARTS = {"norm", "cq", "kr", "glu", "z", "q", "e1"}


def stage_proj(k, l, x_src):
    S, nc = k.S, k.nc
    vec = k.vecs[l]
    w_in = get_w(k, "w_in", l)
    w_uq = get_w(k, "w_uq", l)
    with pool(k) as (sb, ps):
        build_rope(k, sb)
        hT = sb([128, 16, T], BF16, "hT")
        cqn = sb([128, 4, T], BF16, "cqn")
        with pool(k) as (sb1, ps1):
          if "norm" in PROJ_PARTS:
            norm_to_hT(k, sb1, ps1, x_src, vec[:, V_PRE_MIX:V_PRE_MIX + 16], hT)
        with pool(k) as (sb1, ps1):
          if "cq" in PROJ_PARTS:
            raw = sb1([128, 4, T], F32, "raw")
            ckvn = sb1([128, 4, T], BF16, "ckvn")

            def cons_raw(j, tb, pt):
                S.copy("act", raw[:, j, tb * 512:(tb + 1) * 512], pt)
            stream_fm(k, sb1, ps1, w_in, C_CQ, 128, 4, hT, 16, cons_raw)
            norm512(k, sb1, ps1, raw, vec[:, V_QA:V_QA + 4], cqn)
            stream_fm(k, sb1, ps1, w_in, C_CKV, 128, 4, hT, 16, cons_raw, wname="wb2")
            norm512(k, sb1, ps1, raw, vec[:, V_KVA:V_KVA + 4], ckvn)
            for c in range(4):
                S.dma("sp", k.d_kvx_in[c][:, :], ckvn[:, c, :])
        with pool(k) as (sb1, ps1):
          if "kr" in PROJ_PARTS:
            kr = sb1([64, 2, T], F32, "kr")
            krT = sb1([64, T], BF16, "krT")

            def cons_kr(j, tb, pt):
                S.copy("act", kr[:, j, tb * 512:(tb + 1) * 512], pt)
            stream_fm(k, sb1, ps1, w_in, C_KR, 64, 2, hT, 16, cons_kr, m=64)
            S.tt("dve", kr[:, 0, :], kr[:, 0, :], k.cos2[:], ALU.mult)
            S.tt("dve", kr[:, 1, :], kr[:, 1, :], k.sin2[:], ALU.mult)
            S.tt("dve", krT[:], kr[:, 0, :], kr[:, 1, :], ALU.add)
            S.dma("sp", k.d_kvx_in[4][:, :], krT[:])
        with pool(k) as (sb1, ps1):
          if "glu" in PROJ_PARTS:
            abuf = sb1([128, T], F32, "abuf")
            sig = [sb1([128, 512], F32, "sig") for _ in range(2)]
            st = [sb1([128, T], BF16, "st") for _ in range(2)]
            halo = sb1([128, 84], BF16, "halo")

            def cons_glu(j, tb, pt):
                c, which = j // 2, j % 2
                tsl = slice(tb * 512, (tb + 1) * 512)
                if which == 0:
                    S.copy("act", abuf[:, tsl], pt)
                else:
                    S.act(sig[tb % 2][:], pt, AF.Sigmoid)
                    S.tt("dve", st[c % 2][:, tsl], abuf[:, tsl], sig[tb % 2][:], ALU.mult)
                    if tb == NTB - 1:
                        S.dma("act", k.d_hcT[c, :, :], st[c % 2][:])
                        S.copy("dve", halo[:, c * 15:(c + 1) * 15], st[c % 2][:, T - 15:T])
            S_wbs = [sb1([128, 16, 128], BF16, "wbg") for _ in range(3)]
            pss = [ps1([128, 512], F32, "pglu") for _ in range(2)]
            n = 0
            for j in range(8):
                c, which = j // 2, j % 2
                c0 = (C_CA if which == 0 else C_CG) + c * 128
                wb = S_wbs[j % 3]
                S.dma("pool", wb[:], w_in[:, c0:c0 + 128].rearrange("(kc p) n -> p kc n", p=128))
                for tb in range(NTB):
                    pt = pss[n % 2]
                    n += 1
                    for kc in range(16):
                        S.mm(pt[:], wb[:, kc, :], hT[:, kc, tb * 512:(tb + 1) * 512], start=(kc == 0), stop=(kc == 15))
                    cons_glu(j, tb, pt[:])

            def cons_qkv(j, tb, pt):
                tsl = slice(tb * 512, (tb + 1) * 512)
                S.copy("act", st[j % 2][:, tsl], pt)
                if tb == NTB - 1:
                    S.dma("act", k.d_gqkvT[j, :, :], st[j % 2][:])
                    S.copy("dve", halo[:, 60 + j * 2:60 + (j + 1) * 2], st[j % 2][:, T - 2:T])
            stream_fm(k, sb1, ps1, w_in, C_GQ, 128, 12, hT, 16, cons_qkv, wname="wb3")
            S.dma("sp", k.d_halo_in[:, :], halo[:])
        with pool(k) as (sb1, ps1):
          if "z" in PROJ_PARTS:
            wz = sb1([128, 16, 512], BF16, "wz")
            wg = sb1([128, 16, 16], BF16, "wg")
            S.dma("pool", wz[:], w_in[:, C_GZ:C_GZ + 512].rearrange("(kc p) n -> p kc n", p=128))
            S.dma("pool", wg[:], w_in[:, C_GT:C_GT + 16].rearrange("(kc p) n -> p kc n", p=128))
            pz = [ps1([128, 512], F32, "pz") for _ in range(2)]
            pg = [ps1([128, 16], F32, "pg") for _ in range(2)]
            zst = [sb1([128, 512], BF16, "zst") for _ in range(2)]
            gtmp = [sb1([128, 16], F32, "gtmp") for _ in range(2)]
            for tt in range(16):
                tl = slice(tt * 128, (tt + 1) * 128)
                for kc in range(16):
                    S.mm(pz[tt % 2][:], hT[:, kc, tl], wz[:, kc, :], start=(kc == 0), stop=(kc == 15))
                S.act(zst[tt % 2][:], pz[tt % 2][:], AF.Silu)
                S.dma("act", k.d_zs[tl, :], zst[tt % 2][:])
                for kc in range(16):
                    S.mm(pg[tt % 2][:], hT[:, kc, tl], wg[:, kc, :], start=(kc == 0), stop=(kc == 15))
                gsb_ = gtmp[tt % 2]
                S.copy("dve", gsb_[:], pg[tt % 2][:])
                S.ts("dve", k.gates[l % 2][:, tt, 0:8], gsb_[:, 0:8], k.flags[:, 1:2], ALU.mult)
                S.stt("dve", k.gates[l % 2][:, tt, 0:8], gsb_[:, 8:16], k.flags[:, 0:1], k.gates[l % 2][:, tt, 0:8], ALU.mult, ALU.add)
                S.ts("dve", k.gates[l % 2][:, tt, 8:16], gsb_[:, 8:16], k.flags[:, 1:2], ALU.mult)
                S.stt("dve", k.gates[l % 2][:, tt, 8:16], gsb_[:, 0:8], k.flags[:, 0:1], k.gates[l % 2][:, tt, 8:16], ALU.mult, ALU.add)
        with pool(k) as (sb1, ps1):
          if "q" in PROJ_PARTS:
            wq = sb1([128, 4, 2048], BF16, "wq")
            S.dma("pool", wq[:], w_uq.rearrange("(kc p) n -> p kc n", p=128))
            pq = [ps1([128, 512], F32, "pq") for _ in range(2)]
            pr = [ps1([64, 512], F32, "pr") for _ in range(2)]
            qn_st = [sb1([128, T], BF16, "qnst") for _ in range(2)]
            qr_st = [sb1([64, T], BF16, "qrst") for _ in range(2)]
            t1 = sb1([64, 512], F32, "t1")
            t2 = sb1([64, 512], F32, "t2")
            for h in range(8):
                for tb in range(NTB):
                    tsl = slice(tb * 512, (tb + 1) * 512)
                    for c in range(4):
                        S.mm(pq[tb % 2][:], wq[:, c, h * 256:h * 256 + 128], cqn[:, c, tsl], start=(c == 0), stop=(c == 3))
                    S.act(qn_st[h % 2][:, tsl], pq[tb % 2][:], AF.Copy, scale=SCALE)
                    for c in range(4):
                        S.mm(pr[0][:], wq[:, c, h * 256 + 128:h * 256 + 192], cqn[:, c, tsl], start=(c == 0), stop=(c == 3))
                    for c in range(4):
                        S.mm(pr[1][:], wq[:, c, h * 256 + 192:h * 256 + 256], cqn[:, c, tsl], start=(c == 0), stop=(c == 3))
                    S.stt("dve", t1[:], pr[0][:], SCALE, k.cos2[:, tsl], ALU.mult, ALU.mult)
                    S.stt("dve", t2[:], pr[1][:], SCALE, k.sin2[:, tsl], ALU.mult, ALU.mult)
                    S.tt("dve", qr_st[h % 2][:, tsl], t1[:], t2[:], ALU.add)
                S.dma("act", k.d_QnT[h, :, :], qn_st[h % 2][:])
                S.dma("act", k.d_QrT[h, :, :], qr_st[h % 2][:])
    if "e1" not in PROJ_PARTS:
        return
    for c in range(5):
        def _cc(e, c=c):
            return e.collective_compute("AllGather", ALU.bypass, replica_groups=PAIRS,
                                        ins=[k.d_kvx_in[c].opt()], outs=[k.d_kvx_out[c].opt()])
        S.add("pool", _cc, reads=[k.d_kvx_in[c]], writes=[k.d_kvx_out[c]], cc=True)
    S.add("pool", lambda e: e.collective_compute("AllGather", ALU.bypass, replica_groups=PAIRS,
                                                 ins=[k.d_halo_in.opt()], outs=[k.d_halo_out.opt()]),
          reads=[k.d_halo_in], writes=[k.d_halo_out], cc=True)
    for c in range(5):
        k.dbg_copy("kvx_out%d" % c, k.d_kvx_out[c], [256 if c < 4 else 128, T], BF16)
    k.dbg_copy("halo_out", k.d_halo_out, [256, 84], BF16)


PAIRS = [[0, 1], [2, 3], [4, 5], [6, 7]]


def stage_attn(k, l):
    S = k.S
    w_ukv = get_w(k, "w_ukv", l)
    with pool(k) as (sb, ps):
        ckv = sb([128, 4, 2 * T], BF16, "ckv")
        krT = sb([64, 2 * T], BF16, "krTa")
        wkv = sb([128, 4, 2048], BF16, "wkv")
        for rk in range(2):
            for c in range(4):
                S.dma("sp", ckv[:, c, rk * T:(rk + 1) * T], k.d_kvx_out[c][rk * 128:(rk + 1) * 128, :])
            S.dma("sp", krT[:, rk * T:(rk + 1) * T], k.d_kvx_out[4][rk * 64:(rk + 1) * 64, :])
        S.dma("pool", wkv[:], w_ukv.rearrange("(kc p) n -> p kc n", p=128))
        KnT = [sb([128, 2 * T], BF16, "KnT") for _ in range(2)]
        Vh = [sb([128, 32, 128], BF16, "Vh") for _ in range(2)]
        Qn = [sb([128, T], BF16, "Qn") for _ in range(2)]
        Qr = [sb([64, T], BF16, "Qr") for _ in range(2)]
        PT = [sb([128, 512], BF16, "PT") for _ in range(3)]
        rec = sb([128, 512], F32, "rec")
        ost = [sb([128, T], BF16, "ost") for _ in range(2)]
        pk = [ps([128, 512], F32, "pk") for _ in range(2)]
        pst = [ps([128, 512], F32, "pst") for _ in range(2)]
        po = ps([128, 512], F32, "po")
        psm = ps([128, 512], F32, "psm")
        nk = 0
        for h in range(8):
            hb = h % 2
            S.dma("sp", Qn[hb][:], k.d_QnT[h, :, :])
            S.dma("sp", Qr[hb][:], k.d_QrT[h, :, :])
            for kb in range(8):
                pt = pk[nk % 2]
                nk += 1
                for c in range(4):
                    S.mm(pt[:], wkv[:, c, h * 256:h * 256 + 128], ckv[:, c, kb * 512:(kb + 1) * 512],
                         start=(c == 0), stop=(c == 3))
                S.copy("dve" if kb % 2 == 0 else "act", KnT[hb][:, kb * 512:(kb + 1) * 512], pt[:])
            for k4 in range(8):
                pt = pk[nk % 2]
                nk += 1
                for q in range(4):
                    kt = k4 * 4 + q
                    for c in range(4):
                        S.mm(pt[:, q * 128:(q + 1) * 128], ckv[:, c, kt * 128:(kt + 1) * 128],
                             wkv[:, c, h * 256 + 128:h * 256 + 256], start=(c == 0), stop=(c == 3))
                S.copy("dve" if k4 % 2 == 0 else "act",
                       Vh[hb][:, k4 * 4:(k4 + 1) * 4, :], pt[:].rearrange("p (a b) -> p a b", a=4))
            for qb in range(NTB):
                qsl = slice(qb * 512, (qb + 1) * 512)

                def scores(kt):
                    st = pst[kt % 2]
                    S.mm(st[:], KnT[hb][:, kt * 128:(kt + 1) * 128], Qn[hb][:, qsl], start=True, stop=False)
                    S.mm(st[:], krT[:, kt * 128:(kt + 1) * 128], Qr[hb][:, qsl], start=False, stop=True)
                scores(0)
                for kt in range(32):
                    if kt + 1 < 32:
                        scores(kt + 1)
                    p = PT[kt % 3]
                    S.act(p[:], pst[kt % 2][:], AF.Exp)
                    S.mm(po[:], Vh[hb][:, kt, :], p[:], start=(kt == 0), stop=(kt == 31))
                    S.mm(psm[:], k.ones_bf[:], p[:], start=(kt == 0), stop=(kt == 31))
                S.recip(rec[:], psm[:])
                S.tt("dve", ost[hb][:, qsl], po[:], rec[:], ALU.mult)
            S.dma("act", k.d_mixT[h, :, :], ost[hb][:])


def stage_conv(k, l):
    S = k.S
    vec = k.vecs[l]
    with pool(k) as (sb, ps):
        W = 15 + T + 15
        hc = sb([128, 4, W], BF16, "hc")
        D = sb([128, 4, 31, 128], BF16, "Dc")
        y = sb([128, 4, T], F32, "yc")
        hl = sb([128, 2, 84], BF16, "hl")
        hsel = sb([128, 4, 15], F32, "hsel")
        S.memset("dve", hc[:, :, 0:15], 0.0)
        for c in range(4):
            S.dma("sp", hc[:, c, 15:15 + T], k.d_hcT[c, :, :])
        for rk in range(2):
            S.dma("sp", hl[:, rk, :], k.d_halo_out[rk * 128:(rk + 1) * 128, :])
        h0 = hl[:, 0, 0:60].rearrange("p (c j) -> p c j", c=4)
        h1 = hl[:, 1, 0:60].rearrange("p (c j) -> p c j", c=4)
        S.ts("dve", hsel[:], h0, k.flags[:, 0:1], ALU.mult)
        S.stt("dve", hsel[:], h1, k.flags[:, 1:2], hsel[:], ALU.mult, ALU.add)
        for j in range(15):
            S.copy("dve", hc[:, :, 15 + T + j:15 + T + j + 1], hsel[:, :, 14 - j:15 - j])
        n = 0
        for c in range(4):
            for j in range(31):
                eng = ("dve", "pool")[n % 2]
                n += 1
                S.ts(eng, D[:, c, j, :], k.ident_bf[:], vec[:, V_CDW + c * 31 + j:V_CDW + c * 31 + j + 1], ALU.mult)
        pc = [ps([128, 512], F32, "pc") for _ in range(2)]
        n = 0
        for c in range(4):
            for tb in range(NTB):
                pt = pc[n % 2]
                n += 1
                for j in range(31):
                    S.mm(pt[:], D[:, c, j, :], hc[:, c, tb * 512 + j:tb * 512 + j + 512], start=(j == 0), stop=(j == 30))
                S.act(y[:, c, tb * 512:(tb + 1) * 512], pt[:], AF.Identity, bias=vec[:, V_CDB + c:V_CDB + c + 1])
        ybf = sb([128, 4, 512], BF16, "ybf")
        ysq = sb([128, 4, 512], BF16, "ysq")
        p_s = ps([128, 512], F32, "p_s")
        p_q = ps([128, 512], F32, "p_q")
        mean = sb([128, 512], F32, "mean")
        var = sb([128, 512], F32, "var")
        tmp = [sb([128, 512], F32, "ctmp") for _ in range(2)]
        ostg = [sb([128, T], BF16, "costg") for _ in range(4)]
        for tb in range(NTB):
            tsl = slice(tb * 512, (tb + 1) * 512)
            S.copy("act", ybf[:], y[:, :, tsl])
            S.act(ysq[:], y[:, :, tsl], AF.Square)
            for c in range(4):
                S.mm(p_s[:], k.ones_bf[:], ybf[:, c, :], start=(c == 0), stop=(c == 3))
            for c in range(4):
                S.mm(p_q[:], k.ones_bf[:], ysq[:, c, :], start=(c == 0), stop=(c == 3))
            S.act(mean[:], p_s[:], AF.Copy, scale=1.0 / 512.0)
            S.tt("dve", var[:], mean[:], mean[:], ALU.mult)
            S.stt("dve", var[:], p_q[:], 1.0 / 512.0, var[:], ALU.mult, ALU.subtract)
            S.ts("dve", var[:], var[:], 0.0, ALU.max)
            S.act(var[:], var[:], AF.Sqrt, bias=k.eps_col[:, 0:1], scale=1.0)
            S.recip(var[:], var[:])
            for c in range(4):
                t = tmp[c % 2]
                S.tt("dve", t[:], y[:, c, tsl], mean[:], ALU.subtract)
                S.tt("dve", t[:], t[:], var[:], ALU.mult)
                S.ts("dve", t[:], t[:], vec[:, V_CLG + c:V_CLG + c + 1], ALU.mult, vec[:, V_CLB + c:V_CLB + c + 1], ALU.add)
                S.act(ostg[c][:, tsl], t[:], AF.Silu)
        for c in range(4):
            S.dma("act", k.d_mixT[8 + c, :, :], ostg[c][:])


def _bc_last(ap, n):
    return ap.unsqueeze(2).to_broadcast([ap.shape[0], ap.shape[1], n])


def _bc_mid(ap, a):
    return ap.unsqueeze(1).to_broadcast([ap.shape[0], a, ap.shape[1]])


def stage_gdn(k, l):
    S = k.S
    vec = k.vecs[l]
    gp = k.gpar[l]
    gates = k.gates[l % 2]
    C = k.consts
    ident_f = C[:, K_ID:K_ID + 128]
    with pool(k) as (sb, ps):
        QT = sb([128, 4, T], BF16, "gQT")
        KT = sb([128, 4, T], BF16, "gKT")
        Ktm = sb([128, 16, 4, 128], BF16, "gKtm")
        Vtm = sb([128, 16, 4, 128], BF16, "gVtm")
        oacc = sb([128, 16, 4, 128], F32, "goacc")
        gc = sb([128, 8, 16], F32, "ggc")
        egc = sb([128, 8, 16], F32, "gegc")
        beta = sb([128, 8, 16], F32, "gbeta")
        bege = sb([128, 8, 16], F32, "gbege")
        kdx = sb([128, 8, 16], F32, "gkdx")
        decS = [sb([128, 8, 16], F32, "gdecS") for _ in range(2)]
        with pool(k) as (sb1, ps1):
            g = sb1([128, 8, 16], F32, "g")
            xa = sb1([128, 8, 16], F32, "xa")
            negA = sb1([128, 8], F32, "negA")
            for d in range(2):
                S.copy("dve", xa[:, d * 4:(d + 1) * 4, :], gates[:, :, d * 8:d * 8 + 4].rearrange("p t c -> p c t"))
                S.copy("dve", beta[:, d * 4:(d + 1) * 4, :], gates[:, :, d * 8 + 4:d * 8 + 8].rearrange("p t c -> p c t"))
            S.tt("dve", xa[:], xa[:], gp[:, 8:16].unsqueeze(2).to_broadcast([128, 8, 16]), ALU.add)
            S.act(xa[:], xa[:], AF.Exp)
            S.act(xa[:], xa[:], AF.Ln, bias=1.0, scale=1.0)
            S.act(negA[:], gp[:, 0:8], AF.Exp)
            S.ts("dve", negA[:], negA[:], -1.0, ALU.mult)
            S.tt("dve", g[:], xa[:], negA[:].unsqueeze(2).to_broadcast([128, 8, 16]), ALU.mult)
            S.act(beta[:], beta[:], AF.Sigmoid)
            pg_ = ps1([128, 128], F32, "pgc")
            for d in range(2):
                tri = C[:, K_TRIA:K_TRIA + 128] if d == 0 else C[:, K_TRIB:K_TRIB + 128]
                S.mm(pg_[:, d * 64:(d + 1) * 64], tri, g[:, d * 4:(d + 1) * 4, :].rearrange("p a b -> p (a b)"), start=True, stop=True)
            S.copy("dve", gc[:].rearrange("p a b -> p (a b)"), pg_[:])
            S.act(egc[:], gc[:], AF.Exp)
            S.tt("dve", bege[:], beta[:], egc[:], ALU.mult)
            pgs = [ps1([128, 128], F32, "pgs") for _ in range(2)]
            for e in range(2):
                blk = C[:, K_BLK0:K_BLK0 + 128] if e == 0 else C[:, K_BLK1:K_BLK1 + 128]
                S.mm(pgs[e][:], blk, g[:].rearrange("p a b -> p (a b)"), start=True, stop=True)
                S.act(decS[e][:].rearrange("p a b -> p (a b)"), pgs[e][:], AF.Exp)
                rows = slice(e * 64, (e + 1) * 64)
                S.tt("dve", kdx[rows].rearrange("p a b -> p (a b)"), pgs[e][rows], gc[rows].rearrange("p a b -> p (a b)"), ALU.subtract)
            S.act(kdx[:], kdx[:], AF.Exp)
        with pool(k) as (sb1, ps1):
            VT = sb1([128, 4, T], BF16, "gVT")
            Dg = sb1([128, 12, 5, 128], BF16, "Dg")
            xq = [sb1([128, 2 + T + 2], BF16, "xq") for _ in range(2)]
            hl = sb1([128, 2, 84], BF16, "ghl")
            hsel = sb1([128, 12, 2], F32, "ghsel")
            for rk in range(2):
                S.dma("sp", hl[:, rk, :], k.d_halo_out[rk * 128:(rk + 1) * 128, :])
            h0 = hl[:, 0, 60:84].rearrange("p (c j) -> p c j", c=12)
            h1 = hl[:, 1, 60:84].rearrange("p (c j) -> p c j", c=12)
            S.ts("dve", hsel[:], h0, k.flags[:, 0:1], ALU.mult)
            S.stt("dve", hsel[:], h1, k.flags[:, 1:2], hsel[:], ALU.mult, ALU.add)
            n = 0
            for c in range(12):
                for j in range(5):
                    eng = ("dve", "pool")[n % 2]
                    n += 1
                    S.ts(eng, Dg[:, c, j, :], k.ident_bf[:], vec[:, V_GCW + c * 5 + j:V_GCW + c * 5 + j + 1], ALU.mult)
            pcv = [ps1([128, 512], F32, "pcv") for _ in range(2)]
            pss = ps1([128, 512], F32, "pssq")
            sil = [sb1([128, 512], F32, "sil") for _ in range(2)]
            sqb = sb1([128, 512], BF16, "sqb")
            rs = sb1([128, 512], F32, "rs")
            n = 0
            for c in range(12):
                xb = xq[c % 2]
                S.memset("dve", xb[:, 0:2], 0.0)
                S.dma("sp", xb[:, 2:2 + T], k.d_gqkvT[c, :, :])
                for j in range(2):
                    S.copy("dve", xb[:, 2 + T + j:2 + T + j + 1], hsel[:, c, 1 - j:2 - j])
                for tb in range(NTB):
                    tsl = slice(tb * 512, (tb + 1) * 512)
                    pt = pcv[n % 2]
                    sl_ = sil[n % 2]
                    n += 1
                    for j in range(5):
                        S.mm(pt[:], Dg[:, c, j, :], xb[:, tb * 512 + j:tb * 512 + j + 512], start=(j == 0), stop=(j == 4))
                    if c >= 8:
                        S.act(VT[:, c - 8, tsl], pt[:], AF.Silu)
                        continue
                    S.act(sl_[:], pt[:], AF.Silu)
                    S.act(sqb[:], sl_[:], AF.Square)
                    S.mm(pss[:], k.ones_bf[:], sqb[:], start=True, stop=True)
                    S.act(rs[:], pss[:], AF.Sqrt, bias=k.eps_col[:, 0:1], scale=1.0)
                    S.recip(rs[:], rs[:])
                    if c < 4:
                        S.stt("dve", QT[:, c, tsl], sl_[:], 128.0 ** -0.5, rs[:], ALU.mult, ALU.mult)
                    else:
                        S.tt("dve", KT[:, c - 4, tsl], sl_[:], rs[:], ALU.mult)
            ptr = [ps1([128, 4, 128], BF16, "ptr") for _ in range(2)]
            n = 0
            for tt in range(16):
                tl = slice(tt * 128, (tt + 1) * 128)
                for src, dst in ((KT, Ktm), (VT, Vtm)):
                    pt = ptr[n % 2]
                    n += 1
                    for h in range(4):
                        S.transpose(pt[:, h, :], src[:, h, tl], k.ident_bf[:])
                    S.copy("dve" if n % 2 == 0 else "act", dst[:, tt, :, :], pt[:])
        with pool(k) as (sb1, ps1):
            S32 = sb1([128, 4, 128], F32, "S32")
            Sbf = sb1([128, 4, 128], BF16, "Sbf")
            S.memset("dve", S32[:], 0.0)
            S.memset("dve", Sbf[:], 0.0)
            f4 = lambda nm: sb1([128, 4, 128], F32, nm)
            b4 = lambda nm: sb1([128, 4, 128], BF16, nm)
            dg, dd, t1, Es, E2 = f4("dg"), f4("dd"), f4("t1"), f4("Es"), f4("E2")
            Af = f4("Af")
            A_bf = [b4("Abf") for _ in range(2)]
            B_bf = [b4("Bbf") for _ in range(2)]
            P32, Pbf = f4("P32"), b4("Pbf")
            VB, KBg, Kd = b4("VB"), b4("KBg"), b4("Kd")
            U32, WT, QK2 = f4("U32"), b4("WT"), b4("QK2")
            vn, o2, o1 = b4("vn"), f4("o2"), f4("o1")
            pA = ps1([128, 4, 128], F32, "pA")
            pB = ps1([128, 4, 128], F32, "pB")
            pC = ps1([128, 4, 128], F32, "pC")
            pT = ps1([128, 4, 128], BF16, "pT")
            pW = ps1([128, 4, 128], F32, "pW")
            pQ = ps1([128, 4, 128], F32, "pQ")
            pO = ps1([128, 4, 128], F32, "pO")
            pS = ps1([128, 4, 128], F32, "pS")

            def mm4(out, lhs_fn, rhs_fn):
                for h in range(4):
                    S.mm(out[:, h, :], lhs_fn(h), rhs_fn(h), start=True, stop=True)

            def do_tile(D, tt):
                tl = slice(tt * 128, (tt + 1) * 128)
                c0 = D * 4
                m_strict = C[:, K_MSA:K_MSA + 128] if D == 0 else C[:, K_MSB:K_MSB + 128]
                m_incl2 = C[:, K_MIA:K_MIA + 128] if D == 0 else C[:, K_MIB:K_MIB + 128]
                gcc = gc[:, c0:c0 + 4, tt]
                S.tt("dve", dg[:], _bc_mid(ident_f, 4), _bc_last(gcc, 128), ALU.mult)
                mm4(pB, lambda h: k.ones_f[:], lambda h: dg[:, h, :])
                S.tt("dve", dd[:], pB[:], _bc_last(gcc, 128), ALU.subtract)
                S.ts("dve", t1[:], dd[:], 0.0, ALU.max, -1.0, ALU.mult)
                S.act(t1[:], t1[:], AF.Exp)
                S.tt("pool", Es[:], t1[:], _bc_mid(m_strict, 4), ALU.mult)
                S.ts("dve", t1[:], dd[:], 0.0, ALU.min)
                S.act(t1[:], t1[:], AF.Exp)
                S.tt("pool", E2[:], t1[:], _bc_mid(m_incl2, 4), ALU.mult)
                mm4(pA, lambda h: KT[:, h, tl], lambda h: KT[:, h, tl])
                S.tt("dve", Af[:], pA[:], Es[:], ALU.mult)
                S.tt("dve", A_bf[0][:], Af[:], _bc_last(beta[:, c0:c0 + 4, tt], 128), ALU.mult)
                for h in range(4):
                    S.transpose(pT[:, h, :], A_bf[0][:, h, :], k.ident_bf[:])
                S.copy("act", B_bf[0][:], pT[:])
                S.tt("dve", P32[:], _bc_mid(ident_f, 4), B_bf[0][:], ALU.subtract)
                S.copy("act", Pbf[:], P32[:])
                ca, cb = A_bf[0], B_bf[0]
                for lev in range(5):
                    na, nb = A_bf[(lev + 1) % 2], B_bf[(lev + 1) % 2]
                    mm4(pA, lambda h: cb[:, h, :], lambda h: ca[:, h, :])
                    if lev < 4:
                        mm4(pB, lambda h: ca[:, h, :], lambda h: cb[:, h, :])
                    S.copy("act", na[:], pA[:])
                    if lev < 4:
                        S.copy("dve", nb[:], pB[:])
                    mm4(pC, lambda h: na[:, h, :], lambda h: Pbf[:, h, :])
                    S.tt("dve", P32[:], P32[:], pC[:], ALU.add)
                    S.copy("act", Pbf[:], P32[:])
                    ca, cb = na, nb
                S.tt("pool", VB[:], Vtm[:, tt, :, :], _bc_last(beta[:, c0:c0 + 4, tt], 128), ALU.mult)
                S.tt("pool", KBg[:], Ktm[:, tt, :, :], _bc_last(bege[:, c0:c0 + 4, tt], 128), ALU.mult)
                S.tt("pool", Kd[:], Ktm[:, tt, :, :], _bc_last(kdx[:, c0:c0 + 4, tt], 128), ALU.mult)
                mm4(pA, lambda h: Pbf[:, h, :], lambda h: VB[:, h, :])
                S.copy("act", U32[:], pA[:])
                mm4(pW, lambda h: KBg[:, h, :], lambda h: Pbf[:, h, :])
                S.copy("dve", WT[:], pW[:])
                mm4(pQ, lambda h: KT[:, h, tl], lambda h: QT[:, h, tl])
                S.tt("dve", QK2[:], pQ[:], E2[:], ALU.mult)
                order = (0, 1) if D == 0 else (1, 0)
                for e in order:
                    rows = slice(e * 64, (e + 1) * 64)
                    mm4(pW, lambda h: WT[:, h, :], lambda h: Sbf[:, h, :])
                    S.tt("dve", vn[rows], U32[rows], pW[rows], ALU.subtract)
                    mm4(pQ, lambda h: QT[:, h, tl], lambda h: Sbf[:, h, :])
                    mm4(pO, lambda h: QK2[rows, h, :], lambda h: vn[rows, h, :])
                    mm4(pS, lambda h: Kd[rows, h, :], lambda h: vn[rows, h, :])
                    S.tt("dve", S32[:], S32[:], _bc_last(decS[e][:, c0:c0 + 4, tt], 128), ALU.mult)
                    S.tt("dve", S32[:], S32[:], pS[:], ALU.add)
                    S.copy("act", Sbf[:], S32[:])
                    S.copy("act", o2[rows], pO[rows])
                    S.tt("pool" if False else "dve", o1[rows], pQ[rows], _bc_last(egc[rows, c0:c0 + 4, tt], 128), ALU.mult)
                    if D == 0:
                        S.tt("dve", oacc[rows, tt, :, :], o1[rows], o2[rows], ALU.add)
                    else:
                        S.tt("dve", o1[rows], o1[rows], o2[rows], ALU.add)
                        S.tt("dve", oacc[rows, tt, :, :], oacc[rows, tt, :, :], o1[rows], ALU.add)

            for tt in range(16):
                do_tile(0, tt)
            S.dma("sp", k.d_st_in[:, :], S32[:].rearrange("p h v -> p (h v)"))
            S.add("pool", lambda e: e.collective_compute("AllGather", ALU.bypass, replica_groups=PAIRS,
                                                         ins=[k.d_st_in.opt()], outs=[k.d_st_out.opt()]),
                  reads=[k.d_st_in], writes=[k.d_st_out], cc=True)
            stx = sb1([128, 2, 512], F32, "stx")
            for rk in range(2):
                S.dma("sp", stx[:, rk, :], k.d_st_out[rk * 128:(rk + 1) * 128, :])
            S32f = S32[:].rearrange("p h v -> p (h v)")
            S.ts("dve", S32f, stx[:, 0, :], k.flags[:, 0:1], ALU.mult)
            S.stt("dve", S32f, stx[:, 1, :], k.flags[:, 1:2], S32f, ALU.mult, ALU.add)
            S.copy("act", Sbf[:], S32[:])
            for tt in range(15, -1, -1):
                do_tile(1, tt)
        with pool(k) as (sb1, ps1):
            ss = sb1([128, 64], F32, "oss")
            junk = sb1([128, 128], F32, "junk")
            zs = [sb1([128, 4, 128], BF16, "zsb") for _ in range(2)]
            on = [sb1([128, 4, 128], F32, "on") for _ in range(2)]
            onb = [sb1([128, 4, 128], BF16, "onb") for _ in range(2)]
            ostg = sb1([128, 4, T], BF16, "gostg")
            ptr = [ps1([128, 4, 128], BF16, "ptr2") for _ in range(2)]
            for tt in range(16):
                for h in range(4):
                    S.act(junk[:], oacc[:, tt, h, :], AF.Square, accum_out=ss[:, tt * 4 + h:tt * 4 + h + 1])
            S.act(ss[:], ss[:], AF.Sqrt, bias=k.eps_col[:, 0:1], scale=1.0 / 128.0)
            S.recip(ss[:], ss[:])
            for tt in range(16):
                tl = slice(tt * 128, (tt + 1) * 128)
                z = zs[tt % 2]
                S.dma("sp", z[:], k.d_zs[tl, :].rearrange("t (h v) -> t h v", h=4))
                o_ = on[tt % 2]
                S.tt("dve", o_[:], oacc[:, tt, :, :], _bc_last(ss[:, tt * 4:(tt + 1) * 4], 128), ALU.mult)
                S.tt("dve", o_[:], o_[:], _bc_mid(k.gon[l][:], 4), ALU.mult)
                S.tt("dve", onb[tt % 2][:], o_[:], z[:], ALU.mult)
                for h in range(4):
                    S.transpose(ptr[tt % 2][:, h, :], onb[tt % 2][:, h, :], k.ident_bf[:])
                S.copy("act", ostg[:, :, tl], ptr[tt % 2][:])
            for h in range(4):
                S.dma("act", k.d_mixT[12 + h, :, :], ostg[:, h, :])


def stage_out_ffn(k, l, x_src, dst):
    S = k.S
    vec = k.vecs[l]
    w_out = get_w(k, "w_out", l)
    w_gate = get_w(k, "w_gate", l)
    w_up = get_w(k, "w_up", l)
    w_down = get_w(k, "w_down", l)
    d_x1 = k.d_x1
    with pool(k) as (sb, ps):
        wo = sb([128, 16, 2048], BF16, "wo")
        for q in range(4):
            S.dma("pool", wo[:, :, q * 512:(q + 1) * 512],
                  w_out[:, q * 512:(q + 1) * 512].rearrange("(kc p) n -> p kc n", p=128))
        mixb = sb([128, 16, 512], BF16, "mixb")
        xb = sb([128, 16, 512], F32, "xbo")
        mT = sb([128, 16, 512], F32, "mT")
        sq = sb([128, 16, 512], BF16, "sqo")
        h2b = sb([128, 16, 512], BF16, "h2b")
        rstd = sb([128, 512], F32, "rstdo")
        pm = [ps([128, 512], F32, "pm") for _ in range(2)]
        pss = ps([128, 512], F32, "psso")
        for tb in range(NTB):
            tsl = slice(tb * 512, (tb + 1) * 512)
            S.dma("sp", mixb[:], k.d_mixT[:, :, tsl].rearrange("c p t -> p c t"))
            S.dma("sp", xb[:], x_src[:, :, tsl].rearrange("c p t -> p c t"))
            for dch in range(16):
                pt = pm[dch % 2]
                for c in range(16):
                    S.mm(pt[:], wo[:, c, dch * 128:(dch + 1) * 128], mixb[:, c, :], start=(c == 0), stop=(c == 15))
                S.copy("act", mT[:, dch, :], pt[:])
            S.act(sq[:], mT[:], AF.Square)
            for dch in range(16):
                S.mm(pss[:], k.ones_bf[:], sq[:, dch, :], start=(dch == 0), stop=(dch == 15))
            rstd_from_ss(k, rstd[:], pss[:], 1.0 / D_MODEL)
            for dch in range(16):
                S.stt("dve", mT[:, dch, :], mT[:, dch, :], vec[:, V_POST_MIX + dch:V_POST_MIX + dch + 1], rstd[:], ALU.mult, ALU.mult)
                S.tt("dve", xb[:, dch, :], xb[:, dch, :], mT[:, dch, :], ALU.add)
            S.dma("act", d_x1[:, :, tsl].rearrange("c p t -> p c t"), xb[:])
            S.act(sq[:], xb[:], AF.Square)
            for dch in range(16):
                S.mm(pss[:], k.ones_bf[:], sq[:, dch, :], start=(dch == 0), stop=(dch == 15))
            rstd_from_ss(k, rstd[:], pss[:], 1.0 / D_MODEL)
            for dch in range(16):
                S.stt("dve", h2b[:, dch, :], xb[:, dch, :], vec[:, V_PRE_FFN + dch:V_PRE_FFN + dch + 1], rstd[:], ALU.mult, ALU.mult)
            S.dma("act", k.d_h2T[:, :, tsl].rearrange("c p t -> p c t"), h2b[:])
    G = 1024
    for grp in range(T // G):
        g0 = grp * G
        with pool(k) as (sb, ps):
            actT = sb([128, NFF, G], BF16, "actT")
            with pool(k) as (sb1, ps1):
                h2g = sb1([128, 16, G], BF16, "h2g")
                S.dma("sp", h2g[:], k.d_h2T[:, :, g0:g0 + G].rearrange("c p t -> p c t"))
                wg2 = [sb1([128, 16, 256], BF16, "wg2") for _ in range(2)]
                wu2 = [sb1([128, 16, 256], BF16, "wu2") for _ in range(2)]
                pg = [ps1([128, 512], F32, "pg") for _ in range(2)]
                pu = [ps1([128, 512], F32, "pu") for _ in range(2)]
                sg = [sb1([128, 512], F32, "sg") for _ in range(2)]
                n = 0
                for f2 in range(NFF // 2):
                    wg_, wu_ = wg2[f2 % 2], wu2[f2 % 2]
                    S.dma("pool", wg_[:], w_gate[:, f2 * 256:(f2 + 1) * 256].rearrange("(kc p) n -> p kc n", p=128))
                    S.dma("pool", wu_[:], w_up[:, f2 * 256:(f2 + 1) * 256].rearrange("(kc p) n -> p kc n", p=128))
                    for sub in range(2):
                        f = f2 * 2 + sub
                        for tb in range(G // 512):
                            tsl = slice(tb * 512, (tb + 1) * 512)
                            a, b = pg[n % 2], pu[n % 2]
                            s_ = sg[n % 2]
                            n += 1
                            for kc in range(16):
                                S.mm(a[:], wg_[:, kc, sub * 128:(sub + 1) * 128], h2g[:, kc, tsl], start=(kc == 0), stop=(kc == 15))
                            for kc in range(16):
                                S.mm(b[:], wu_[:, kc, sub * 128:(sub + 1) * 128], h2g[:, kc, tsl], start=(kc == 0), stop=(kc == 15))
                            S.act(s_[:], a[:], AF.Silu)
                            S.tt("dve", actT[:, f, tsl], s_[:], b[:], ALU.mult)
            rstdf = [sb([128, 512], F32, "rstdf") for _ in range(G // 512)]
            with pool(k) as (sb1, ps1):
                wd2 = [sb1([128, NFF, 256], BF16, "wd2") for _ in range(2)]
                pf = [ps1([128, 512], F32, "pf") for _ in range(2)]
                pssf = [ps1([128, 512], F32, "pssf") for _ in range(G // 512)]
                fst = [sb1([128, 512], F32, "fst") for _ in range(3)]
                sqf = [sb1([128, 512], BF16, "sqf") for _ in range(2)]
                n = 0
                for d2 in range(8):
                    wd_ = wd2[d2 % 2]
                    S.dma("pool", wd_[:], w_down[:, d2 * 256:(d2 + 1) * 256].rearrange("(fc p) n -> p fc n", p=128))
                    for sub in range(2):
                        dch = d2 * 2 + sub
                        for tb in range(G // 512):
                            tsl = slice(tb * 512, (tb + 1) * 512)
                            pt = pf[n % 2]
                            fs = fst[n % 3]
                            sf = sqf[n % 2]
                            n += 1
                            for f in range(NFF):
                                S.mm(pt[:], wd_[:, f, sub * 128:(sub + 1) * 128], actT[:, f, tsl], start=(f == 0), stop=(f == NFF - 1))
                            S.copy("act", fs[:], pt[:])
                            S.dma("act", k.d_fT[dch, :, g0 + tb * 512:g0 + (tb + 1) * 512], fs[:])
                            S.act(sf[:], pt[:], AF.Square)
                            S.mm(pssf[tb][:], k.ones_bf[:], sf[:], start=(dch == 0), stop=(dch == 15))
                for tb in range(G // 512):
                    rstd_from_ss(k, rstdf[tb][:], pssf[tb][:], 1.0 / D_MODEL)
            with pool(k) as (sb1, ps1):
                fb = sb1([128, 16, 512], F32, "fb")
                x1b = sb1([128, 16, 512], F32, "x1b")
                for tb in range(G // 512):
                    tsl = slice(g0 + tb * 512, g0 + (tb + 1) * 512)
                    S.dma("sp", fb[:], k.d_fT[:, :, tsl].rearrange("c p t -> p c t"))
                    S.dma("sp", x1b[:], d_x1[:, :, tsl].rearrange("c p t -> p c t"))
                    for dch in range(16):
                        S.stt("dve", fb[:, dch, :], fb[:, dch, :], vec[:, V_POST_FFN + dch:V_POST_FFN + dch + 1], rstdf[tb][:], ALU.mult, ALU.mult)
                        S.tt("dve", x1b[:, dch, :], x1b[:, dch, :], fb[:, dch, :], ALU.add)
                    S.dma("act", dst[:, :, tsl].rearrange("c p t -> p c t"), x1b[:])


_NC_CACHE = {}
WSHAPE = {"w_in": (D_MODEL, W_IN_COLS), "w_uq": (512, 2048), "w_ukv": (512, 2048), "w_out": (2048, 2048),
          "w_gate": (D_MODEL, D_FF), "w_up": (D_MODEL, D_FF), "w_down": (D_FF, D_MODEL)}
ALL8 = [[0, 1, 2, 3, 4, 5, 6, 7]]


def get_w(k, name, l):
    key = (name, l)
    if key not in k.wfull:
        nc, S = k.nc, k.S
        rows, cols = WSHAPE[name]
        nm = "%s%d" % (name, l)
        d_in = nc.dram_tensor(nm + "_s", [rows // 8, cols], F32, kind="ExternalInput").ap()
        d_b = nc.dram_tensor(nm + "_b", [rows // 8, cols], F32, kind="Internal").ap()
        d_full = nc.dram_tensor(nm + "_f", [rows, cols], F32, kind="Internal").ap()
        S.dma("sp", d_b, d_in)
        S.add("pool", lambda e: e.collective_compute("AllGather", ALU.bypass, replica_groups=ALL8,
                                                     ins=[d_b.opt()], outs=[d_full.opt()]),
              reads=[d_b], writes=[d_full], cc=True)
        k.wfull[key] = d_full
        k.used_inputs.add(nm + "_s")
    return k.wfull[key]


def prefetch_w(k, l, names):
    for n in names:
        get_w(k, n, l)


def build_program(stages=("all",), debug=(), feed=()):
    nc = bass.Bass("TRN2", target_bir_lowering=False)
    k = Ctx()
    k.nc = nc
    k.S = Sched(nc)
    k.uid = 0

    def din(name, shape, dt=F32):
        return nc.dram_tensor(name, list(shape), dt, kind="ExternalInput").ap()

    NO_IO = ("halo_in", "halo_out", "st_in", "st_out") + tuple("kvx_in%d" % c for c in range(5)) + tuple("kvx_out%d" % c for c in range(5))

    def dscr(name, shape, dt):
        kind = "ExternalOutput" if (name in debug and name not in NO_IO) else "Internal"
        if name in feed:
            kind = "ExternalInput"
        return nc.dram_tensor(name, list(shape), dt, kind=kind).ap()

    def dbg_copy(name, src, shape, dt):
        if name in debug:
            d = nc.dram_tensor("dbg_" + name, list(shape), dt, kind="ExternalOutput").ap()
            k.S.dma("sp", d, src)
    k.dbg_copy = dbg_copy

    k.d_xT = din("xT", [16, 128, T])
    k.d_pos = din("pos", [64, T], I32)
    k.d_consts = din("consts", [128, NC])
    k.d_flags = din("flags", [128, 2])
    k.d_vecs = din("vecs", [DEPTH, 128, NV])
    k.d_gpar = din("gpar", [DEPTH, 128, 16])
    k.d_gon = din("gon", [DEPTH, 128, 128])
    k.wfull = {}
    k.used_inputs = set()
    k.d_out = nc.dram_tensor("outT", [16, 128, T], F32, kind="ExternalOutput").ap()
    k.d_xres = dscr("xres", [16, 128, T], F32)
    k.d_kvx_in = [dscr("kvx_in%d" % c, [128 if c < 4 else 64, T], BF16) for c in range(5)]
    k.d_kvx_out = [dscr("kvx_out%d" % c, [256 if c < 4 else 128, T], BF16) for c in range(5)]
    k.d_halo_in = dscr("halo_in", [128, 84], BF16)
    k.d_halo_out = dscr("halo_out", [256, 84], BF16)
    k.d_hcT = dscr("hcT", [4, 128, T], BF16)
    k.d_gqkvT = dscr("gqkvT", [12, 128, T], BF16)
    k.d_zs = dscr("zs", [T, 512], BF16)
    k.d_QnT = dscr("QnT", [8, 128, T], BF16)
    k.d_QrT = dscr("QrT", [8, 64, T], BF16)
    k.d_mixT = dscr("mixT", [16, 128, T], BF16)
    k.d_st_in = dscr("st_in", [128, 512], F32)
    k.d_st_out = dscr("st_out", [256, 512], F32)
    k.d_h2T = dscr("h2T", [16, 128, T], BF16)
    k.d_fT = dscr("fT", [16, 128, T], F32)
    k.d_x1 = dscr("x1", [16, 128, T], F32)

    with contextlib.ExitStack() as ges:
        def gsb(shape, dt, name):
            return ges.enter_context(nc.sbuf_tensor("g_" + name, list(shape), dt))
        k.gsb = gsb
        k.gates = [gsb([128, 16, 16], F32, "gates%d" % i) for i in range(2)]
        stage_setup(k)
        allst = "all" in stages
        x_src = k.d_xT
        for l in range(DEPTH):
            if allst:
                prefetch_w(k, l, ("w_in", "w_uq", "w_ukv"))
            if allst or ("proj%d" % l) in stages:
                stage_proj(k, l, x_src)
            if allst:
                prefetch_w(k, l, ("w_out", "w_gate", "w_up", "w_down"))
            if allst or ("attn%d" % l) in stages:
                stage_attn(k, l)
            if allst or ("conv%d" % l) in stages:
                stage_conv(k, l)
            if allst or ("gdn%d" % l) in stages:
                stage_gdn(k, l)
            if allst or ("out%d" % l) in stages:
                stage_out_ffn(k, l, x_src, k.d_out if l == DEPTH - 1 else k.d_xres)
            x_src = k.d_xres
        k.S.emit()
    _NC_CACHE["used"] = set(k.used_inputs)
    return nc


def _consts():
    c = np.zeros((128, NC), np.float32)
    p = np.arange(128)
    blk = (p[:, None] // 64) == (p[None, :] // 64)
    c[:, K_ID:K_ID + 128] = np.eye(128, dtype=np.float32)
    c[:, K_TRIA:K_TRIA + 128] = (blk & (p[:, None] <= p[None, :]))
    c[:, K_TRIB:K_TRIB + 128] = (blk & (p[:, None] >= p[None, :]))
    c[:, K_BLK0:K_BLK0 + 128] = (p[:, None] < 64) & np.ones((1, 128), bool)
    c[:, K_BLK1:K_BLK1 + 128] = (p[:, None] >= 64) & np.ones((1, 128), bool)
    c[:, K_MSA:K_MSA + 128] = (blk & (p[:, None] > p[None, :]))
    c[:, K_MIA:K_MIA + 128] = (blk & (p[None, :] >= p[:, None]))
    c[:, K_MSB:K_MSB + 128] = (blk & (p[:, None] < p[None, :]))
    c[:, K_MIB:K_MIB + 128] = (blk & (p[None, :] <= p[:, None]))
    f = (p % 32).astype(np.float32)
    c[:, K_INVF] = (1.0 / (10000.0 ** (np.arange(0, 64, 2, dtype=np.float32) / 64.0)))[p % 32]
    c[:, K_SGN] = np.where((p % 64) < 32, -1.0, 1.0)
    return c


def _fm(v):
    v = np.asarray(v, np.float32)
    return np.ascontiguousarray(v.reshape(-1, 128).T)


def prep_inputs(inp):
    f32 = np.float32
    L = DEPTH
    w_in = np.asarray(inp["w_in"], f32)
    o_cq, o_ckv, o_kr, o_conv, o_gq, o_gk, o_gv, o_gz, o_g = 0, 512, 1024, 1088, 2112, 2624, 3136, 3648, 4160
    rot = np.concatenate([np.arange(32, 64), np.arange(0, 32)])
    cols = np.concatenate([
        np.arange(o_cq, o_cq + 512), np.arange(o_ckv, o_ckv + 512),
        np.arange(o_kr, o_kr + 64), o_kr + rot,
        np.arange(o_conv, o_conv + 1024),
        np.arange(o_gq, o_gq + 512), np.arange(o_gk, o_gk + 512), np.arange(o_gv, o_gv + 512),
        np.arange(o_gz, o_gz + 512), np.arange(o_g, o_g + 16)])
    w_in_r = w_in[:, :, cols]
    w_uq = np.asarray(inp["w_uq"], f32)
    qcols = []
    for h in range(8):
        b = h * 192
        qcols += [np.arange(b, b + 192), b + 128 + rot]
    w_uq_r = np.ascontiguousarray(w_uq[:, :, np.concatenate(qcols)])
    consts = _consts()
    vecs = []
    gpar = []
    gon = []
    for r in range(2):
        vr = np.zeros((L, 128, NV), f32)
        gr = np.zeros((L, 128, 16), f32)
        for l in range(L):
            vr[l, :, V_PRE_MIX:V_PRE_MIX + 16] = _fm(inp["pre_mix_norm"][l])
            vr[l, :, V_POST_MIX:V_POST_MIX + 16] = _fm(inp["post_mix_norm"][l])
            vr[l, :, V_PRE_FFN:V_PRE_FFN + 16] = _fm(inp["pre_ffn_norm"][l])
            vr[l, :, V_POST_FFN:V_POST_FFN + 16] = _fm(inp["post_ffn_norm"][l])
            vr[l, :, V_QA:V_QA + 4] = _fm(inp["q_a_norm"][l])
            vr[l, :, V_KVA:V_KVA + 4] = _fm(inp["kv_a_norm"][l])
            vr[l, :, V_CDB:V_CDB + 4] = _fm(inp["conv_dw_b"][l])
            vr[l, :, V_CLG:V_CLG + 4] = _fm(inp["conv_ln_g"][l])
            vr[l, :, V_CLB:V_CLB + 4] = _fm(inp["conv_ln_b"][l])
            cw = np.asarray(inp["conv_dw_w"][l], f32)[:, 0, :]
            gw = np.asarray(inp["gdn_conv_w"][l], f32)[:, 0, :]
            if r == 1:
                cw = cw[::-1]
                gw = gw[::-1]
            for c in range(4):
                vr[l, :, V_CDW + c * 31:V_CDW + (c + 1) * 31] = cw[:, c * 128:(c + 1) * 128].T
            for c in range(12):
                vr[l, :, V_GCW + c * 5:V_GCW + (c + 1) * 5] = gw[:, c * 128:(c + 1) * 128].T
            al = np.asarray(inp["gdn_a_log"][l], f32)
            db = np.asarray(inp["gdn_dt_bias"][l], f32)
            if r == 1:
                al = al[::-1]
                db = db[::-1]
            gr[l, :, 0:8] = al.reshape(1, 8)
            gr[l, :, 8:16] = db.reshape(1, 8)
        vecs.append(vr)
        gpar.append(gr)
    gon = np.ascontiguousarray(np.broadcast_to(np.asarray(inp["gdn_out_norm"], f32)[:, None, :], (L, 128, 128)))
    x = np.asarray(inp["x"], f32)
    pos = np.asarray(inp["positions"], np.int32)
    shared = {"consts": consts, "gon": gon}
    wts = {"w_in": w_in_r, "w_uq": w_uq_r, "w_ukv": np.asarray(inp["w_ukv"], f32),
           "w_out": np.asarray(inp["w_out"], f32), "w_gate": np.asarray(inp["w_gate"], f32),
           "w_up": np.asarray(inp["w_up"], f32), "w_down": np.asarray(inp["w_down"], f32)}
    used = _NC_CACHE.get("used")
    in_maps = []
    for core in range(8):
        b, r = core // 2, core % 2
        sl = slice(0, T) if r == 0 else slice(T, SEQ)
        xs = x[b, sl]
        ps_ = pos[b, sl]
        if r == 1:
            xs = xs[::-1]
            ps_ = ps_[::-1]
        m = dict(shared)
        m["xT"] = np.ascontiguousarray(xs.T).reshape(16, 128, T)
        m["pos"] = np.ascontiguousarray(np.broadcast_to(ps_[None, :], (64, T))).astype(np.int32)
        fl = np.zeros((128, 2), f32)
        fl[:, 1 - r] = 1.0
        m["flags"] = fl
        m["vecs"] = vecs[r]
        m["gpar"] = gpar[r]
        for name, w in wts.items():
            rows = w.shape[1] // 8
            for l in range(L):
                nm = "%s%d_s" % (name, l)
                if used is None or nm in used:
                    m[nm] = np.ascontiguousarray(w[l, core * rows:(core + 1) * rows, :])
        in_maps.append(m)
    return in_maps


def kernel(**inputs):
    if "nc" not in _NC_CACHE:
        _NC_CACHE["nc"] = build_program()
    nc = _NC_CACHE["nc"]
    in_maps = prep_inputs(inputs)
    res = run_bass_kernel_spmd(nc, in_maps, core_ids=list(range(8)))
    out = np.zeros((BATCH, SEQ, D_MODEL), np.float32)
    for core in range(8):
        b, r = core // 2, core % 2
        o = np.asarray(res.results[core]["outT"]).reshape(D_MODEL, T).T
        if r == 1:
            out[b, T:] = o[::-1]
        else:
            out[b, :T] = o
    return out
```
